# Optimizing a Trainium2 kernel written in Bass

```python
import jax, jax.numpy as jnp
from jax import lax
import numpy as np

D_MODEL = 1024
BATCH = 4
SEQ = 4096
DEPTH = 1

GRID_W = 64

NA_HEADS = 8
NA_HEAD_DIM = 64
NA_WIDTH = NA_HEADS * NA_HEAD_DIM
NA_WIN_ROWS_MAX = 8
NA_WIN_COLS = 16

MLA_HEADS = 8
MLA_Q_LORA = 256
MLA_KV_LORA = 128
MLA_NOPE_DIM = 64
MLA_ROPE_DIM = 32
MLA_QK_DIM = MLA_NOPE_DIM + MLA_ROPE_DIM
MLA_V_DIM = 64
MLA_WIDTH = MLA_HEADS * MLA_V_DIM
MLA_Q_BLOCK = 128
ROPE_THETA = 10000.0

IN_SIZES = (NA_WIDTH, NA_WIDTH, NA_WIDTH, MLA_Q_LORA, MLA_KV_LORA, MLA_ROPE_DIM, D_MODEL, D_MODEL)
IN_TOTAL = 3 * NA_WIDTH + MLA_Q_LORA + MLA_KV_LORA + MLA_ROPE_DIM + 2 * D_MODEL

FFN_HIDDEN = -(-8 * D_MODEL // (3 * 256)) * 256

DN_ALPHA = (2.0 * DEPTH) ** 0.25
DN_BETA = (8.0 * DEPTH) ** -0.25
LN_EPS = 1e-5
RMS_EPS = 1e-6

kernel_name = "hybrid_na2d_mla_gated_deepnorm_encoder"


def layer_norm(x, g, b):
    xf = x.astype(jnp.float32)
    mu = jnp.mean(xf, axis=-1, keepdims=True)
    var = jnp.mean(jnp.square(xf - mu), axis=-1, keepdims=True)
    y = (xf - mu) * lax.rsqrt(var + LN_EPS) * g.astype(jnp.float32) + b.astype(jnp.float32)
    return y.astype(x.dtype)


def rms_norm(x, g):
    xf = x.astype(jnp.float32)
    y = xf * lax.rsqrt(jnp.mean(jnp.square(xf), axis=-1, keepdims=True) + RMS_EPS) * g.astype(jnp.float32)
    return y.astype(x.dtype)


def axial_rope_tables(seq_len):
    t = jnp.arange(seq_len)
    rows = (t // GRID_W).astype(jnp.float32)
    cols = (t % GRID_W).astype(jnp.float32)
    half = MLA_ROPE_DIM // 2
    inv_freq = ROPE_THETA ** (-jnp.arange(0, half, 2, dtype=jnp.float32) / half)
    ang_r = rows[:, None, None] * inv_freq
    ang_c = cols[:, None, None] * inv_freq
    return jnp.cos(ang_r), jnp.sin(ang_r), jnp.cos(ang_c), jnp.sin(ang_c)


def rotate_pairs(x, cos, sin):
    x1, x2 = jnp.split(x, 2, axis=-1)
    return jnp.concatenate([x1 * cos - x2 * sin, x2 * cos + x1 * sin], axis=-1)


def apply_axial_rope(x, tables):
    cos_r, sin_r, cos_c, sin_c = tables
    xf = x.astype(jnp.float32)
    xr, xc = jnp.split(xf, 2, axis=-1)
    out = jnp.concatenate([rotate_pairs(xr, cos_r, sin_r), rotate_pairs(xc, cos_c, sin_c)], axis=-1)
    return out.astype(x.dtype)


def neighbourhood_attention_2d(q, k, v, rpb):
    B, S, H, dh = q.shape
    rows = S // GRID_W
    kr = min(NA_WIN_ROWS_MAX, rows)
    kc = NA_WIN_COLS
    qg = q.reshape(B, rows, GRID_W, H, dh).transpose(1, 0, 2, 3, 4)
    kg = k.reshape(B, rows, GRID_W, H, dh)
    vg = v.reshape(B, rows, GRID_W, H, dh)
    col = jnp.arange(GRID_W)
    col_start = jnp.clip(col - kc // 2, 0, GRID_W - kc)
    col_idx = col_start[:, None] + jnp.arange(kc)[None, :]
    col_bias_idx = col_idx - col[:, None] + (NA_WIN_COLS - 1)
    rpb_cols = rpb.astype(jnp.float32)[:, :, col_bias_idx]
    scale = dh ** -0.5

    def one_row(args):
        r, q_row = args
        r_start = jnp.clip(r - kr // 2, 0, rows - kr)
        k_band = lax.dynamic_slice_in_dim(kg, r_start, kr, axis=1)
        v_band = lax.dynamic_slice_in_dim(vg, r_start, kr, axis=1)
        k_nb = k_band[:, :, col_idx]
        v_nb = v_band[:, :, col_idx]
        row_bias_idx = r_start + jnp.arange(kr) - r + (NA_WIN_ROWS_MAX - 1)
        bias = rpb_cols[:, row_bias_idx].transpose(0, 2, 1, 3)
        s = jnp.einsum('bqhd,brqchd->bhqrc', q_row, k_nb).astype(jnp.float32) * scale + bias[None]
        p = jax.nn.softmax(s.reshape(B, H, GRID_W, kr * kc), axis=-1).reshape(B, H, GRID_W, kr, kc)
        return jnp.einsum('bhqrc,brqchd->bqhd', p.astype(v.dtype), v_nb)

    out = lax.map(one_row, (jnp.arange(rows), qg))
    return out.transpose(1, 0, 2, 3, 4).reshape(B, S, H * dh)


def latent_attention(q_lat, kv_lat, k_rope, q_norm_g, w_uq, kv_norm_g, w_ukv, tables):
    B, S, _ = q_lat.shape
    H = MLA_HEADS
    q = (rms_norm(q_lat, q_norm_g) @ w_uq).reshape(B, S, H, MLA_QK_DIM)
    q_nope, q_pe = jnp.split(q, [MLA_NOPE_DIM], axis=-1)
    q = jnp.concatenate([q_nope, apply_axial_rope(q_pe, tables)], axis=-1)
    kv = (rms_norm(kv_lat, kv_norm_g) @ w_ukv).reshape(B, S, H, MLA_NOPE_DIM + MLA_V_DIM)
    k_nope, v = jnp.split(kv, [MLA_NOPE_DIM], axis=-1)
    k_pe = apply_axial_rope(k_rope[:, :, None, :], tables)
    k = jnp.concatenate([k_nope, jnp.broadcast_to(k_pe, (B, S, H, MLA_ROPE_DIM))], axis=-1)
    scale = MLA_QK_DIM ** -0.5
    n_blk = S // MLA_Q_BLOCK
    qb = q.reshape(B, n_blk, MLA_Q_BLOCK, H, MLA_QK_DIM).transpose(1, 0, 2, 3, 4)

    def one_block(q_blk):
        s = jnp.einsum('bqhd,bkhd->bhqk', q_blk, k).astype(jnp.float32) * scale
        p = jax.nn.softmax(s, axis=-1)
        return jnp.einsum('bhqk,bkhd->bqhd', p.astype(v.dtype), v)

    o = lax.map(one_block, qb)
    return o.transpose(1, 0, 2, 3, 4).reshape(B, S, H * MLA_V_DIM)


def hybrid_mixer(x, w_in, b_gate, na_rpb, mla_q_norm, mla_w_uq, mla_kv_norm, mla_w_ukv,
                 w_branch_na, w_branch_mla, w_out, tables):
    B, S, _ = x.shape
    h = x @ w_in
    offs = [int(o) for o in np.cumsum(IN_SIZES)[:-1]]
    q_na, k_na, v_na, q_lat, kv_lat, k_rope, g_a, g_b = jnp.split(h, offs, axis=-1)
    shp = (B, S, NA_HEADS, NA_HEAD_DIM)
    y_a = neighbourhood_attention_2d(q_na.reshape(shp), k_na.reshape(shp), v_na.reshape(shp), na_rpb) @ w_branch_na
    y_b = latent_attention(q_lat, kv_lat, k_rope, mla_q_norm, mla_w_uq, mla_kv_norm, mla_w_ukv, tables) @ w_branch_mla
    b_a, b_b = jnp.split(b_gate, 2, axis=-1)
    merged = jax.nn.sigmoid(g_a + b_a) * y_a + jax.nn.sigmoid(g_b + b_b) * y_b
    return merged @ w_out


def swiglu(x, w_ffn_in, w_ffn_out):
    gate, up = jnp.split(x @ w_ffn_in, 2, axis=-1)
    return (jax.nn.silu(gate) * up) @ w_ffn_out


def setup_inputs(seed: int = 0) -> dict:
    key = jax.random.key(seed)
    ks = jax.random.split(key, 18)
    nrm = jax.random.normal
    f32 = jnp.float32
    L = DEPTH
    return {
        "x": nrm(ks[0], (BATCH, SEQ, D_MODEL), f32),
        "w_in": nrm(ks[1], (L, D_MODEL, IN_TOTAL), f32) * D_MODEL ** -0.5,
        "b_gate": nrm(ks[2], (L, 2 * D_MODEL), f32) * 0.01,
        "na_rpb": nrm(ks[3], (L, NA_HEADS, 2 * NA_WIN_ROWS_MAX - 1, 2 * NA_WIN_COLS - 1), f32) * 0.02,
        "mla_q_norm": 1.0 + 0.01 * nrm(ks[4], (L, MLA_Q_LORA), f32),
        "mla_w_uq": nrm(ks[5], (L, MLA_Q_LORA, MLA_HEADS * MLA_QK_DIM), f32) * MLA_Q_LORA ** -0.5,
        "mla_kv_norm": 1.0 + 0.01 * nrm(ks[6], (L, MLA_KV_LORA), f32),
        "mla_w_ukv": nrm(ks[7], (L, MLA_KV_LORA, MLA_HEADS * (MLA_NOPE_DIM + MLA_V_DIM)), f32) * MLA_KV_LORA ** -0.5,
        "w_branch_na": nrm(ks[8], (L, NA_WIDTH, D_MODEL), f32) * (NA_WIDTH ** -0.5 * DN_BETA),
        "w_branch_mla": nrm(ks[9], (L, MLA_WIDTH, D_MODEL), f32) * (MLA_WIDTH ** -0.5 * DN_BETA),
        "w_out": nrm(ks[10], (L, D_MODEL, D_MODEL), f32) * (D_MODEL ** -0.5 * DN_BETA),
        "ln1_g": 1.0 + 0.01 * nrm(ks[11], (L, D_MODEL), f32),
        "ln1_b": 0.01 * nrm(ks[12], (L, D_MODEL), f32),
        "w_ffn_in": nrm(ks[13], (L, D_MODEL, 2 * FFN_HIDDEN), f32) * D_MODEL ** -0.5,
        "w_ffn_out": nrm(ks[14], (L, FFN_HIDDEN, D_MODEL), f32) * (FFN_HIDDEN ** -0.5 * DN_BETA),
        "ln2_g": 1.0 + 0.01 * nrm(ks[15], (L, D_MODEL), f32),
        "ln2_b": 0.01 * nrm(ks[16], (L, D_MODEL), f32),
    }


def reference(x, w_in, b_gate, na_rpb, mla_q_norm, mla_w_uq, mla_kv_norm, mla_w_ukv,
              w_branch_na, w_branch_mla, w_out, ln1_g, ln1_b, w_ffn_in, w_ffn_out, ln2_g, ln2_b):
    tables = axial_rope_tables(x.shape[1])
    for l in range(DEPTH):
        m = hybrid_mixer(x, w_in[l], b_gate[l], na_rpb[l], mla_q_norm[l], mla_w_uq[l],
                         mla_kv_norm[l], mla_w_ukv[l], w_branch_na[l], w_branch_mla[l], w_out[l], tables)
        x = layer_norm(DN_ALPHA * x + m, ln1_g[l], ln1_b[l])
        x = layer_norm(DN_ALPHA * x + swiglu(x, w_ffn_in[l], w_ffn_out[l]), ln2_g[l], ln2_b[l])
    return x
```

```python
import numpy as np
from contextlib import ExitStack
import concourse.bass as bass
import concourse.mybir as mybir
from concourse.bass_utils import run_bass_kernel_spmd

F32 = mybir.dt.float32
BF16 = mybir.dt.bfloat16
AF = mybir.ActivationFunctionType
ALU = mybir.AluOpType

ENGS = ("pe", "act", "dve", "pool", "sp")

D = 1024
S = 4096
NT = 2048
NKEY = 2304
FH = 2816
NJ = 22
ALPHA = float(2.0 ** 0.25)
NA_SCALE = 0.125
MLA_SCALE = float(96 ** -0.5)
MASK_NEG = -30000.0


class Trk:
    __slots__ = ("w", "r")

    def __init__(self):
        self.w = None
        self.r = {}


class Op:
    __slots__ = ("eng", "fn", "deps", "is_dma", "sem", "val", "signal", "idx", "waits", "clock")

    def __init__(self, eng, fn, is_dma=False):
        self.eng = eng
        self.fn = fn
        self.deps = []
        self.is_dma = is_dma
        self.sem = None
        self.val = 0
        self.signal = False
        self.idx = 0
        self.waits = []
        self.clock = None

    def key(self):
        return ("dma", self.sem) if self.is_dma else ("eng", self.eng)


class Prog:
    def __init__(self):
        self.ops = []
        self.dma_counts = {}
        self.last_eng = {}
        self.last_dma = {}

    def _deps(self, op, reads, writes):
        deps = {}
        for t in reads:
            if t.w is not None:
                deps[id(t.w)] = t.w
        for t in writes:
            if t.w is not None:
                deps[id(t.w)] = t.w
            for d in t.r.values():
                deps[id(d)] = d
        k = op.key()
        for t in reads:
            t.r[k] = op
        for t in writes:
            t.w = op
            t.r = {}
        deps.pop(id(op), None)
        op.deps = list(deps.values())

    def add(self, eng, fn, reads=(), writes=()):
        op = Op(eng, fn)
        self._deps(op, reads, writes)
        self.ops.append(op)
        self.last_eng[eng] = op
        return op

    def dma(self, eng, fn, semkey, reads=(), writes=(), n=1):
        op = Op(eng, fn, is_dma=True)
        op.sem = semkey
        self.dma_counts[semkey] = self.dma_counts.get(semkey, 0) + 16 * n
        op.val = self.dma_counts[semkey]
        self._deps(op, reads, writes)
        self.ops.append(op)
        self.last_dma[semkey] = op
        return op

    def barrier(self):
        lasts = list(self.last_eng.values()) + list(self.last_dma.values())
        for e in ENGS:
            op = Op(e, lambda eng: eng.nop())
            op.deps = [d for d in lasts]
            self.ops.append(op)

    def resolve(self):
        cnt = {e: 0 for e in ENGS}
        for op in self.ops:
            if not op.is_dma:
                cnt[op.eng] += 1
                op.idx = cnt[op.eng]
        know = {e: {} for e in ENGS}
        for op in self.ops:
            k = know[op.eng]
            waits = []
            for d in op.deps:
                key = d.key()
                val = d.val if d.is_dma else d.idx
                if (not d.is_dma) and d.eng == op.eng and d.eng == "pe":
                    continue
                if k.get(key, 0) >= val:
                    continue
                waits.append(d)
                for kk, vv in d.clock.items():
                    if k.get(kk, 0) < vv:
                        k[kk] = vv
                if k.get(key, 0) < val:
                    k[key] = val
            op.waits = waits
            for d in waits:
                d.signal = True
            op.clock = dict(k)
            if op.is_dma:
                op.clock[("dma", op.sem)] = op.val
            else:
                op.clock[("eng", op.eng)] = op.idx
        cnt = {e: 0 for e in ENGS}
        for op in self.ops:
            if not op.is_dma and op.signal:
                cnt[op.eng] += 1
                op.val = cnt[op.eng]

    def emit(self, nc, sems):
        per = {e: [] for e in ENGS}
        for op in self.ops:
            per[op.eng].append(op)

        def run(engine, ops):
            for op in ops:
                ws = {}
                for d in op.waits:
                    key = d.key()
                    if ws.get(key, 0) < d.val:
                        ws[key] = d.val
                for key, v in ws.items():
                    engine.wait_ge(sems[key], v)
                ins = op.fn(engine)
                if op.is_dma:
                    if not isinstance(ins, (list, tuple)):
                        ins = [ins]
                    for i in ins:
                        i.then_inc(sems[("dma", op.sem)], 16)
                elif op.signal:
                    ins.then_inc(sems[("eng", op.eng)], 1)

        with nc.Block() as block:
            @block.tensor
            def _(e):
                run(e, per["pe"])

            @block.scalar
            def _(e):
                run(e, per["act"])

            @block.vector
            def _(e):
                run(e, per["dve"])

            @block.gpsimd
            def _(e):
                run(e, per["pool"])

            @block.sync
            def _(e):
                run(e, per["sp"])


class Ring:
    def __init__(self, items):
        self.items = items
        self.i = 0

    def next(self):
        it = self.items[self.i % len(self.items)]
        self.i += 1
        return it


ARENA_BYTES = 206 * 1024


def na_chunks(m):
    if m == 0:
        return [0, 1, 2, 3], 0
    if m == 1:
        return [0, 1, 2, 3], 4
    return [m - 2, m - 1, m, m + 1, m + 2], 8


def build_nc(debug=False):
    nc = bass.Bass("TRN2", target_bir_lowering=False)

    def dram_in(name, shape, dt=F32):
        return nc.dram_tensor(name, list(shape), dt, kind="ExternalInput").ap()

    x = dram_in("x", [S, D])
    rope = dram_in("rope", [2, 32, S])
    w_in_a = dram_in("w_in_a", [128, 8, 1984])
    w_in_g = dram_in("w_in_g", [128, 8, 2048])
    w_uq = dram_in("w_uq", [128, 2, 8, 128])
    w_uk = dram_in("w_uk", [128, 8, 64])
    w_uv = dram_in("w_uv", [128, 512])
    small = dram_in("small", [128, 32])
    wbna_d = dram_in("wbna", [128, 4, D])
    wbmla_d = dram_in("wbmla", [128, 4, D])
    wout_d = dram_in("wout", [128, 8, D])
    ln_d = dram_in("ln", [4, D])
    wf1_d = dram_in("wf1", [NJ, 128, 8, 256])
    wf2_d = dram_in("wf2", [128, NJ, D])
    nab_d = dram_in("nab", [8, 128, 13, 128])
    ident_d = dram_in("ident", [128, 128])
    y = nc.dram_tensor("y", [NT, D], F32, kind="ExternalOutput").ap()
    xb = nc.dram_tensor("xb", [S, D], BF16, kind="Internal").ap()
    x1s = nc.dram_tensor("x1s", [NT, D], F32, kind="Internal").ap()
    dbg = {}
    if debug:
        dbg["na"] = nc.dram_tensor("dbg_na", [128, 4, NT], F32, kind="ExternalOutput").ap()
        dbg["mla"] = nc.dram_tensor("dbg_mla", [128, 4, NT], F32, kind="ExternalOutput").ap()
        dbg["x1"] = nc.dram_tensor("dbg_x1", [NT, D], F32, kind="ExternalOutput").ap()

    P = Prog()
    es = ExitStack()
    with es:
        arena = es.enter_context(nc.sbuf_tensor("arena", [128, ARENA_BYTES // 2], BF16))
        psum = es.enter_context(nc.psum_tensor("psum", [128, 4096], F32))
        state = {"off": 0, "hw": 0}

        def alloc(shape, dt):
            n = int(np.prod(shape[1:]))
            esz = 4 if dt == F32 else 2
            nb = n * esz
            off = state["off"]
            assert off % 64 == 0
            assert off + nb <= ARENA_BYTES, ("arena overflow", off, nb)
            ap = arena[:, off // 2: (off + nb) // 2]
            if dt == F32:
                ap = ap.bitcast(F32)
            if len(shape) == 3:
                ap = ap.rearrange("p (a b) -> p a b", a=shape[1])
            elif len(shape) == 4:
                ap = ap.rearrange("p (a b c) -> p a b c", a=shape[1], b=shape[2])
            state["off"] = off + (nb + 63) // 64 * 64
            state["hw"] = max(state["hw"], state["off"])
            return ap

        def bank(b):
            return psum[:, b * 512:(b + 1) * 512]

        bank_t = [Trk() for _ in range(8)]

        def bring(ids):
            return Ring([(bank(b), bank_t[b]) for b in ids])

        def mm(out, lhsT, rhs, start, stop, reads, writes):
            P.add("pe", lambda e: e.matmul(out, lhsT=lhsT, rhs=rhs, start=start, stop=stop), reads, writes)

        def act(out, in_, func, reads, writes, bias=None, scale=None):
            kw = {}
            if bias is not None:
                kw["bias"] = bias
            if scale is not None:
                kw["scale"] = scale
            P.add("act", lambda e: e.activation(out=out, in_=in_, func=func, **kw), reads, writes)

        def tt(eng, out, in0, in1, op, reads, writes):
            P.add(eng, lambda e: e.tensor_tensor(out=out, in0=in0, in1=in1, op=op), reads, writes)

        def stt(out, in0, scalar, in1, op0, op1, reads, writes):
            P.add("dve", lambda e: e.scalar_tensor_tensor(out=out, in0=in0, scalar=scalar, in1=in1, op0=op0, op1=op1),
                  reads, writes)

        def ts(eng, out, in0, s1, s2, op0, op1, reads, writes):
            if s2 is None:
                P.add(eng, lambda e: e.tensor_scalar(out=out, in0=in0, scalar1=s1, scalar2=None, op0=op0), reads, writes)
            else:
                P.add(eng, lambda e: e.tensor_scalar(out=out, in0=in0, scalar1=s1, scalar2=s2, op0=op0, op1=op1),
                      reads, writes)

        def copy(eng, out, in_, reads, writes):
            if eng == "act":
                act(out, in_, AF.Copy, reads, writes)
            else:
                P.add(eng, lambda e: e.tensor_copy(out=out, in_=in_), reads, writes)

        def dma(eng, out, in_, key, reads, writes):
            P.dma(eng, lambda e: e.dma_start(out=out, in_=in_), key, reads, writes)

        cp_ring = Ring(["act", "dve"])

        ident = alloc([128, 128], BF16); t_ident = Trk()
        ones_f = alloc([128, 64], F32); t_ones_f = Trk()
        ones_b = alloc([128, 128], BF16); t_ones_b = Trk()
        nhalf = alloc([128, 512], F32); t_nhalf = Trk()
        small_sb = alloc([128, 32], F32); t_small = Trk()
        na_outT = alloc([128, 4, NT], BF16); t_na_out = [Trk() for _ in range(4)]
        mla_outT = alloc([128, 4, NT], BF16); t_mla_out = [Trk() for _ in range(4)]
        U0 = state["off"]
        qlatT = alloc([128, 2, NT], BF16); t_qlat = [Trk() for _ in range(4)]
        ckvT = alloc([128, S], BF16); t_ckv = [Trk() for _ in range(8)]
        kpeT = alloc([128, S], BF16); t_kpe = [Trk() for _ in range(8)]
        den_r = Ring([(alloc([128, 512], F32), Trk()) for _ in range(2)])
        rec_r = Ring([(alloc([128, 512], F32), Trk()) for _ in range(2)])
        U1 = state["off"]

        dma("pool", ident, ident_d, "ident", [], [t_ident])
        dma("sp", small_sb, small, "small", [], [t_small])
        P.add("pool", lambda e: e.memset(ones_f, 1.0), [], [t_ones_f])
        P.add("pool", lambda e: e.memset(ones_b, 1.0), [], [t_ones_b])
        P.add("pool", lambda e: e.memset(nhalf, -0.5), [], [t_nhalf])

        t_xb = [Trk() for _ in range(8)]
        for tc in range(8):
            dma("pool", xb[tc * 512:(tc + 1) * 512, :], x[tc * 512:(tc + 1) * 512, :], ("xb", tc), [], [t_xb[tc]])

        def normalize(obank, t_obank, dst, t_dst):
            den, t_den = den_r.next()
            rec, t_rec = rec_r.next()
            act(den[64:65, :], obank[64:65, :], AF.Copy, [t_obank], [t_den])
            bc, t_bc = bank(6), bank_t[6]
            mm(bc[0:64, :], ones_f[64:65, 0:64], den[64:65, :], True, True, [t_ones_f, t_den], [t_bc])
            P.add("dve", lambda e: e.reciprocal(out=rec[0:64, :], in_=bc[0:64, :]), [t_bc], [t_rec])
            tt("dve", dst, obank[0:64, :], rec[0:64, :], ALU.mult, [t_obank, t_rec], [t_dst])

        state["off"] = U1
        wa = alloc([128, 8, 1984], BF16); t_wa = [Trk() for _ in range(4)]
        xT_r = Ring([(alloc([128, 8, 512], BF16), Trk(), i) for i in range(2)])
        qnaT = alloc([128, 4, NT], BF16); t_qna = Trk()
        knaT = alloc([128, 4, NKEY], BF16); t_kna = Trk()
        vna = alloc([128, 18, 8, 65], BF16); t_vna = Trk()
        ropek_r = Ring([(alloc([128, 2, 512], F32), Trk(), i) for i in range(2)])
        stage_r = Ring([(alloc([128, 512], F32), Trk()) for _ in range(4)])
        sq_r = Ring([(alloc([128, 2, 512], BF16), Trk()) for _ in range(2)])
        rstd_r = Ring([(alloc([128, 512], F32), Trk()) for _ in range(2)])
        tmp_r = Ring([(alloc([128, 512], F32), Trk()) for _ in range(4)])
        A_END = state["off"]

        wa_cols = [(0, 512), (512, 1024), (1024, 1536), (1536, 1984)]
        for i, (c0, c1) in enumerate(wa_cols):
            dma("pool", wa[:, :, c0:c1], w_in_a[:, :, c0:c1], ("wa", i), [], [t_wa[i]])
        P.add("pool", lambda e: e.memset(vna[:, :, :, 64:65], 1.0), [], [t_vna])

        pa = bring([0, 1, 2, 3, 4, 5, 7])

        def proj_fm(xT, t_xT, col0, M, ntok, t_w):
            pb, t_pb = pa.next()
            for c in range(8):
                mm(pb[0:M, 0:ntok], wa[:, c, col0:col0 + M], xT[:, c, 0:ntok], c == 0, c == 7, [t_w, t_xT], [t_pb])
            return pb, t_pb

        def rms_bcast(sq, t_sq, nch, dim):
            pb, t_pb = pa.next()
            for c in range(nch):
                mm(pb, ones_b, sq[:, c, :], c == 0, c == nch - 1, [t_ones_b, t_sq], [t_pb])
            tmp, t_tmp = tmp_r.next()
            ts("dve", tmp, pb, 1.0 / dim, 1e-6, ALU.mult, ALU.add, [t_pb], [t_tmp])
            rstd, t_rstd = rstd_r.next()
            tt("pool", rstd, tmp, nhalf, ALU.pow, [t_tmp, t_nhalf], [t_rstd])
            return rstd, t_rstd

        for tc in range(8):
            xT, t_xT, xi = xT_r.next()
            P.dma("sp", lambda e, xT=xT, tc=tc: [e.dma_start_transpose(out=xT[:, c, :], in_=xb[tc * 512:(tc + 1) * 512, c * 128:(c + 1) * 128])
                                                 for c in range(8)],
                  ("xT", xi), [t_xb[tc]], [t_xT], n=8)
            rk, t_rk, ri = ropek_r.next()
            P.dma("sp", lambda e, rk=rk, tc=tc: [e.dma_start(out=rk[0:32, 0, :], in_=rope[0, :, tc * 512:(tc + 1) * 512]),
                                                 e.dma_start(out=rk[0:32, 1, :], in_=rope[1, :, tc * 512:(tc + 1) * 512])],
                  ("ropek", ri), [], [t_rk], n=2)
            own = tc < 4
            nkey = 512 if tc < 4 else (256 if tc == 4 else 0)
            tok0 = tc * 512
            if own:
                for fc in range(4):
                    pb, t_pb = proj_fm(xT, t_xT, fc * 128, 128, 512, t_wa[0])
                    copy(cp_ring.next(), qnaT[:, fc, tok0:tok0 + 512], pb, [t_pb], [t_qna])
                sq, t_sq = sq_r.next()
                stg = []
                for fc in range(2):
                    pb, t_pb = proj_fm(xT, t_xT, 1536 + fc * 128, 128, 512, t_wa[3])
                    st_, t_st = stage_r.next()
                    act(st_, pb, AF.Copy, [t_pb], [t_st])
                    tt("dve", sq[:, fc, :], pb, st_, ALU.mult, [t_pb, t_st], [t_sq])
                    stg.append((st_, t_st))
                rstd, t_rstd = rms_bcast(sq, t_sq, 2, 256.0)
                for fc in range(2):
                    st_, t_st = stg[fc]
                    stt(qlatT[:, fc, tok0:tok0 + 512], st_, small_sb[:, fc:fc + 1], rstd, ALU.mult, ALU.mult,
                        [t_st, t_small, t_rstd], [t_qlat[tc]])
            if nkey:
                for fc in range(4):
                    pb, t_pb = proj_fm(xT, t_xT, 512 + fc * 128, 128, nkey, t_wa[1])
                    copy(cp_ring.next(), knaT[:, fc, tok0:tok0 + nkey], pb[:, 0:nkey], [t_pb], [t_kna])
                for tl in range(nkey // 128):
                    pb, t_pb = pa.next()
                    for c in range(8):
                        mm(pb, xT[:, c, tl * 128:(tl + 1) * 128], wa[:, c, 1024:1536], c == 0, c == 7, [t_wa[2], t_xT], [t_pb])
                    copy(cp_ring.next(), vna[:, tc * 4 + tl, :, 0:64], pb.rearrange("p (h d) -> p h d", h=8), [t_pb], [t_vna])
            pb, t_pb = proj_fm(xT, t_xT, 1792, 128, 512, t_wa[3])
            st_, t_st = stage_r.next()
            act(st_, pb, AF.Copy, [t_pb], [t_st])
            sq, t_sq = sq_r.next()
            tt("dve", sq[:, 0, :], pb, st_, ALU.mult, [t_pb, t_st], [t_sq])
            rstd, t_rstd = rms_bcast(sq, t_sq, 1, 128.0)
            stt(ckvT[:, tok0:tok0 + 512], st_, small_sb[:, 2:3], rstd, ALU.mult, ALU.mult, [t_st, t_small, t_rstd], [t_ckv[tc]])
            pb1, t_pb1 = proj_fm(xT, t_xT, 1920, 32, 512, t_wa[3])
            pb2, t_pb2 = proj_fm(xT, t_xT, 1952, 32, 512, t_wa[3])
            t1, t_t1 = tmp_r.next()
            t2, t_t2 = tmp_r.next()
            tt("dve", t1[64:96, :], pb1[0:32, :], rk[0:32, 0, :], ALU.mult, [t_pb1, t_rk], [t_t1])
            tt("dve", t2[64:96, :], pb2[0:32, :], rk[0:32, 1, :], ALU.mult, [t_pb2, t_rk], [t_t2])
            tt("pool", kpeT[64:96, tok0:tok0 + 512], t1[64:96, :], t2[64:96, :], ALU.add, [t_t1, t_t2], [t_kpe[tc]])

        P.barrier()
        state["off"] = U1
        nab_r = Ring([(alloc([128, 13, 128], F32), Trk(), i) for i in range(2)])
        natmp_r = Ring([(alloc([128, 640], F32), Trk()) for _ in range(2)])
        nap_r = Ring([(alloc([128, 640], BF16), Trk()) for _ in range(2)])
        assert state["off"] <= U1 + 8 * 1984 * 2 + 2 * 8 * 512 * 2, "NA scratch must fit in dead wa+xT region"

        naS_r = Ring([(psum[:, 0:1024], [bank_t[0], bank_t[1]]), (psum[:, 1024:2048], [bank_t[2], bank_t[3]])])
        naO_r = bring([4, 5])
        for h in range(8):
            hc, hp = h // 2, h % 2
            p0 = hp * 64
            nb, t_nb, ni = nab_r.next()
            dma("sp", nb, nab_d[h], ("nab", ni), [], [t_nb])
            for qg in range(4):
                ob, t_ob = naO_r.next()
                for mmi in range(4):
                    m = qg * 4 + mmi
                    chunks, pt0 = na_chunks(m)
                    n = len(chunks)
                    sb_, t_sb = naS_r.next()
                    for i, j in enumerate(chunks):
                        mm(sb_[:, i * 128:(i + 1) * 128], knaT[p0:p0 + 64, hc, j * 128:(j + 1) * 128],
                           qnaT[p0:p0 + 64, hc, m * 128:(m + 1) * 128], True, True, [t_kna, t_qna], t_sb)
                    tmp, t_tmp = natmp_r.next()
                    stt(tmp[:, 0:n * 128], sb_[:, 0:n * 128], NA_SCALE,
                        nb[:, pt0:pt0 + n, :].rearrange("p a b -> p (a b)"), ALU.mult, ALU.add, t_sb + [t_nb], [t_tmp])
                    pt, t_pt = nap_r.next()
                    act(pt[:, 0:n * 128], tmp[:, 0:n * 128], AF.Exp, [t_tmp], [t_pt])
                    for i, j in enumerate(chunks):
                        mm(ob[0:65, mmi * 128:(mmi + 1) * 128], vna[:, j, h, 0:65], pt[:, i * 128:(i + 1) * 128],
                           i == 0, i == n - 1, [t_vna, t_pt], [t_ob])
                normalize(ob, t_ob, na_outT[p0:p0 + 64, hc, qg * 512:(qg + 1) * 512], t_na_out[qg])

        P.barrier()
        state["off"] = U1
        wuq = alloc([128, 2, 8, 128], BF16); t_wuq = Trk()
        wuk = alloc([128, 8, 64], BF16); t_wuk = Trk()
        wuv = alloc([128, 512], BF16); t_wuv = Trk()
        ropeq = alloc([128, 2, NT], F32); t_ropeq = Trk()
        vml = alloc([128, 32, 8, 65], BF16); t_vml = Trk()
        qh_r = Ring([(alloc([128, NT], BF16), Trk()) for _ in range(2)])
        kh_r = Ring([(alloc([128, S], BF16), Trk()) for _ in range(2)])
        pT_r = Ring([(alloc([128, 512], BF16), Trk()) for _ in range(3)])
        rt_r = Ring([(alloc([128, 512], F32), Trk()) for _ in range(4)])

        dma("pool", wuq, w_uq, "wuq", [], [t_wuq])
        dma("pool", wuk, w_uk, "wuk", [], [t_wuk])
        dma("pool", wuv, w_uv, "wuv", [], [t_wuv])
        P.dma("sp", lambda e: [e.dma_start(out=ropeq[64:96, 0, :], in_=rope[0, :, 0:NT]),
                               e.dma_start(out=ropeq[0:32, 1, :], in_=rope[1, :, 0:NT])], "ropeq", [], [t_ropeq], n=2)
        P.add("pool", lambda e: e.memset(vml[:, :, :, 64:65], 1.0), [], [t_vml])

        pm = bring([3, 7])
        for tl in range(32):
            pb, t_pb = pm.next()
            mm(pb, ckvT[:, tl * 128:(tl + 1) * 128], wuv, True, True, [t_ckv[tl // 4], t_wuv], [t_pb])
            copy(cp_ring.next(), vml[:, tl, :, 0:64], pb.rearrange("p (h d) -> p h d", h=8), [t_pb], [t_vml])

        def mla_proj(h):
            qh, t_qh = qh_r.next()
            kh, t_kh = kh_r.next()
            for tc in range(8):
                pb, t_pb = pm.next()
                mm(pb[0:64, :], wuk[:, h, :], ckvT[:, tc * 512:(tc + 1) * 512], True, True, [t_wuk, t_ckv[tc]], [t_pb])
                copy(cp_ring.next(), kh[0:64, tc * 512:(tc + 1) * 512], pb[0:64, :], [t_pb], [t_kh])
            P.add("pool", lambda e: e.tensor_copy(out=kh[64:96, :], in_=kpeT[64:96, :]), t_kpe, [t_kh])
            for t in range(4):
                pbA, t_pbA = pm.next()
                for c in range(2):
                    mm(pbA[0:96, :], wuq[:, c, h, 0:96], qlatT[:, c, t * 512:(t + 1) * 512], c == 0, c == 1, [t_wuq, t_qlat[t]], [t_pbA])
                pbB, t_pbB = pm.next()
                for c in range(2):
                    mm(pbB[0:32, :], wuq[:, c, h, 96:128], qlatT[:, c, t * 512:(t + 1) * 512], c == 0, c == 1, [t_wuq, t_qlat[t]], [t_pbB])
                act(qh[0:64, t * 512:(t + 1) * 512], pbA[0:64, :], AF.Copy, [t_pbA], [t_qh])
                t1, t_t1 = rt_r.next()
                t2, t_t2 = rt_r.next()
                tt("dve", t1[64:96, :], pbA[64:96, :], ropeq[64:96, 0, t * 512:(t + 1) * 512], ALU.mult, [t_pbA, t_ropeq], [t_t1])
                tt("dve", t2[64:96, :], pbB[0:32, :], ropeq[0:32, 1, t * 512:(t + 1) * 512], ALU.mult, [t_pbB, t_ropeq], [t_t2])
                tt("pool", qh[64:96, t * 512:(t + 1) * 512], t1[64:96, :], t2[64:96, :], ALU.add, [t_t1, t_t2], [t_qh])
            return qh, t_qh, kh, t_kh

        mS_r = bring([0, 1, 2])
        mO_r = bring([4, 5])
        nxt = mla_proj(0)
        for h in range(8):
            qh, t_qh, kh, t_kh = nxt
            if h + 1 < 8:
                nxt = mla_proj(h + 1)
            hc, hp = h // 2, h % 2
            p0 = hp * 64
            for qc in range(4):
                ob, t_ob = mO_r.next()
                pend = []

                def issue_s(kc):
                    sb_, t_sb = mS_r.next()
                    mm(sb_, kh[0:96, kc * 128:(kc + 1) * 128], qh[0:96, qc * 512:(qc + 1) * 512], True, True, [t_kh, t_qh], [t_sb])
                    pt, t_pt = pT_r.next()
                    act(pt, sb_, AF.Exp, [t_sb], [t_pt], scale=MLA_SCALE)
                    pend.append((kc, pt, t_pt))

                issue_s(0)
                issue_s(1)
                for kc in range(32):
                    if kc + 2 < 32:
                        issue_s(kc + 2)
                    k_, pt, t_pt = pend.pop(0)
                    mm(ob[0:65, :], vml[:, k_, h, 0:65], pt, k_ == 0, k_ == 31, [t_vml, t_pt], [t_ob])
                normalize(ob, t_ob, mla_outT[p0:p0 + 64, hc, qc * 512:(qc + 1) * 512], t_mla_out[qc])

        if debug:
            for c in range(4):
                dma("pool", dbg["na"][:, c, :], na_outT[:, c, :], ("dbgna", c), t_na_out, [Trk()])
                dma("pool", dbg["mla"][:, c, :], mla_outT[:, c, :], ("dbgmla", c), t_mla_out, [Trk()])

        P.barrier()
        state["off"] = U0
        x1T = alloc([128, 8, NT], BF16); t_x1T = [Trk() for _ in range(4)]
        CD0 = state["off"]
        wbna = alloc([128, 4, D], BF16); t_wbna = Trk()
        wbmla = alloc([128, 4, D], BF16); t_wbmla = Trk()
        wg = alloc([128, 8, 2048], BF16); t_wg = [Trk() for _ in range(2)]
        wout = alloc([128, 8, D], BF16); t_wout = Trk()
        ln1 = alloc([128, 2, D], F32); t_ln1 = Trk()
        xT2_r = Ring([(alloc([128, 8, 512], BF16), Trk(), i) for i in range(2)])
        mg_r = Ring([(alloc([128, 8, 512], BF16), Trk()) for _ in range(2)])
        sg_r = Ring([(alloc([128, 512], F32), Trk()) for _ in range(4)])
        xt_r = Ring([(alloc([128, D], F32), Trk(), i) for i in range(2)])
        z_r = Ring([(alloc([128, D], F32), Trk(), i) for i in range(2)])
        x1b_r = Ring([(alloc([128, D], BF16), Trk()) for _ in range(2)])
        st_r = Ring([(alloc([128, 2, 6], F32), Trk()) for _ in range(2)])
        mv_r = Ring([(alloc([128, 4], F32), Trk()) for _ in range(2)])

        dma("pool", wbna, wbna_d, "wbna", [], [t_wbna])
        dma("pool", wbmla, wbmla_d, "wbmla", [], [t_wbmla])
        dma("pool", wg[:, :, 0:1024], w_in_g[:, :, 0:1024], "wg0", [], [t_wg[0]])
        dma("pool", wg[:, :, 1024:2048], w_in_g[:, :, 1024:2048], "wg1", [], [t_wg[1]])
        dma("pool", wout, wout_d, "wout", [], [t_wout])
        P.dma("sp", lambda e: [e.dma_start(out=ln1[:, 0, :], in_=ln_d[0].partition_broadcast(128)),
                               e.dma_start(out=ln1[:, 1, :], in_=ln_d[1].partition_broadcast(128))], "ln1", [], [t_ln1], n=2)

        def layer_norm(z, t_z, lnp, t_lnp):
            st_, t_st = st_r.next()
            mv, t_mv = mv_r.next()
            for i in range(2):
                P.add("dve", lambda e, i=i: e.bn_stats(out=st_[:, i, :], in_=z[:, i * 512:(i + 1) * 512]), [t_z], [t_st])
            P.add("dve", lambda e: e.bn_aggr(out=mv[:, 0:2], in_=st_.rearrange("p a b -> p (a b)")), [t_st], [t_mv])
            P.add("dve", lambda e: e.tensor_scalar_add(out=mv[:, 2:3], in0=mv[:, 1:2], scalar1=1e-5), [t_mv], [t_mv])
            tt("pool", mv[:, 3:4], mv[:, 2:3], nhalf[:, 0:1], ALU.pow, [t_mv, t_nhalf], [t_mv])
            ts("dve", z, z, mv[:, 0:1], mv[:, 3:4], ALU.subtract, ALU.mult, [t_z, t_mv], [t_z])
            tt("pool", z, z, lnp[:, 0, :], ALU.mult, [t_z, t_lnp], [t_z])
            tt("dve", z, z, lnp[:, 1, :], ALU.add, [t_z, t_lnp], [t_z])

        pc = bring([0, 1, 2, 3, 7])
        po_r = Ring([(4, 5)])
        for t in range(4):
            xT, t_xT, xi = xT2_r.next()
            P.dma("sp", lambda e, xT=xT, t=t: [e.dma_start_transpose(out=xT[:, c, :], in_=xb[t * 512:(t + 1) * 512, c * 128:(c + 1) * 128])
                                               for c in range(8)],
                  ("xT2", xi), [t_xb[t]], [t_xT], n=8)
            mg, t_mg = mg_r.next()
            for f in range(8):
                ya, t_ya = pc.next()
                for c in range(4):
                    mm(ya, wbna[:, c, f * 128:(f + 1) * 128], na_outT[:, c, t * 512:(t + 1) * 512], c == 0, c == 3, [t_wbna, t_na_out[t]], [t_ya])
                yb, t_yb = pc.next()
                for c in range(4):
                    mm(yb, wbmla[:, c, f * 128:(f + 1) * 128], mla_outT[:, c, t * 512:(t + 1) * 512], c == 0, c == 3, [t_wbmla, t_mla_out[t]], [t_yb])
                ga, t_ga = pc.next()
                for c in range(8):
                    mm(ga, wg[:, c, f * 128:(f + 1) * 128], xT[:, c, :], c == 0, c == 7, [t_wg[0], t_xT], [t_ga])
                gb, t_gb = pc.next()
                for c in range(8):
                    mm(gb, wg[:, c, 1024 + f * 128:1024 + (f + 1) * 128], xT[:, c, :], c == 0, c == 7, [t_wg[1], t_xT], [t_gb])
                sa, t_sa = sg_r.next()
                sb2, t_sb2 = sg_r.next()
                act(sa, ga, AF.Sigmoid, [t_ga, t_small], [t_sa], bias=small_sb[:, 3 + f:4 + f])
                act(sb2, gb, AF.Sigmoid, [t_gb, t_small], [t_sb2], bias=small_sb[:, 11 + f:12 + f])
                tt("dve", sa, ya, sa, ALU.mult, [t_ya, t_sa], [t_sa])
                tt("dve", sb2, yb, sb2, ALU.mult, [t_yb, t_sb2], [t_sb2])
                tt("pool", mg[:, f, :], sa, sb2, ALU.add, [t_sa, t_sb2], [t_mg])
            for tl in range(4):
                tile = t * 4 + tl
                b0, b1 = po_r.next()
                for n_, b in enumerate((b0, b1)):
                    for f in range(8):
                        mm(bank(b), mg[:, f, tl * 128:(tl + 1) * 128], wout[:, f, n_ * 512:(n_ + 1) * 512], f == 0, f == 7,
                           [t_mg, t_wout], [bank_t[b]])
                xt, t_xt, xti = xt_r.next()
                dma("sp", xt, x[tile * 128:(tile + 1) * 128, :], ("xt", xti), [], [t_xt])
                z, t_z, zi = z_r.next()
                for n_, b in enumerate((b0, b1)):
                    stt(z[:, n_ * 512:(n_ + 1) * 512], xt[:, n_ * 512:(n_ + 1) * 512], ALPHA, bank(b), ALU.mult, ALU.add,
                        [t_xt, bank_t[b]], [t_z])
                layer_norm(z, t_z, ln1, t_ln1)
                t_x1s = Trk()
                dma("sp", x1s[tile * 128:(tile + 1) * 128, :], z, ("x1st", zi), [t_z], [t_x1s])
                if debug:
                    dma("sp", dbg["x1"][tile * 128:(tile + 1) * 128, :], z, ("dbgx1", zi), [t_z], [Trk()])
                x1b, t_x1b = x1b_r.next()
                act(x1b, z, AF.Copy, [t_z], [t_x1b])
                tp, t_tp = bank(6), bank_t[6]
                tpb = tp.bitcast(BF16)
                for c in range(8):
                    P.add("pe", lambda e, c=c, tpb=tpb, x1b=x1b: e.transpose(out=tpb[:, c * 128:(c + 1) * 128], in_=x1b[:, c * 128:(c + 1) * 128], identity=ident),
                          [t_x1b, t_ident], [t_tp])
                copy("dve", x1T[:, :, tile * 128:(tile + 1) * 128], tpb.rearrange("p (c t) -> p c t", c=8), [t_tp], [t_x1T[t]])

        P.barrier()
        state["off"] = CD0
        wf2 = alloc([128, NJ, D], BF16); t_wf2 = [Trk() for _ in range(2)]
        hT = alloc([128, NJ, 1024], BF16); t_hT = [Trk() for _ in range(2)]
        ln2 = alloc([128, 2, D], F32); t_ln2 = Trk()
        wf1_r = Ring([(alloc([128, 8, 256], BF16), Trk(), i) for i in range(3)])
        sl_r = Ring([(alloc([128, 512], F32), Trk()) for _ in range(2)])
        x1_r = Ring([(alloc([128, D], F32), Trk(), i) for i in range(2)])
        z2_r = Ring([(alloc([128, D], F32), Trk(), i) for i in range(2)])
        st_r = Ring([(alloc([128, 2, 6], F32), Trk()) for _ in range(2)])
        mv_r = Ring([(alloc([128, 4], F32), Trk()) for _ in range(2)])

        P.dma("sp", lambda e: [e.dma_start(out=ln2[:, 0, :], in_=ln_d[2].partition_broadcast(128)),
                               e.dma_start(out=ln2[:, 1, :], in_=ln_d[3].partition_broadcast(128))], "ln2", [], [t_ln2], n=2)
        pg = bring([0, 1, 2, 3])
        po_r = Ring([(4, 5), (6, 7)])
        wf1_tiles = {}

        def load_wf1(hb, j):
            w, t_w, wi = wf1_r.next()
            dma("pool", w, wf1_d[j], ("wf1", wi), [], [t_w])
            wf1_tiles[(hb, j)] = (w, t_w)

        for hb in range(2):
            load_wf1(hb, 0)
            load_wf1(hb, 1)
            if hb == 0:
                dma("pool", wf2[:, 0:11, :], wf2_d[:, 0:11, :], "wf2a", [], [t_wf2[0]])
                dma("pool", wf2[:, 11:22, :], wf2_d[:, 11:22, :], "wf2b", [], [t_wf2[1]])
            for j in range(NJ):
                if j + 2 < NJ:
                    load_wf1(hb, j + 2)
                w, t_w = wf1_tiles.pop((hb, j))
                for tq in range(2):
                    t = hb * 2 + tq
                    g, t_g = pg.next()
                    for c in range(8):
                        mm(g, w[:, c, 0:128], x1T[:, c, t * 512:(t + 1) * 512], c == 0, c == 7, [t_w, t_x1T[t]], [t_g])
                    u, t_u = pg.next()
                    for c in range(8):
                        mm(u, w[:, c, 128:256], x1T[:, c, t * 512:(t + 1) * 512], c == 0, c == 7, [t_w, t_x1T[t]], [t_u])
                    sl, t_sl = sl_r.next()
                    act(sl, g, AF.Silu, [t_g], [t_sl])
                    tt("dve", hT[:, j, tq * 512:(tq + 1) * 512], sl, u, ALU.mult, [t_sl, t_u], [t_hT[tq]])
            for tl in range(8):
                tile = hb * 8 + tl
                b0, b1 = po_r.next()
                for n_, b in enumerate((b0, b1)):
                    for j in range(NJ):
                        mm(bank(b), hT[:, j, tl * 128:(tl + 1) * 128], wf2[:, j, n_ * 512:(n_ + 1) * 512], j == 0, j == NJ - 1,
                           [t_hT[tl // 4], t_wf2[j // 11]], [bank_t[b]])
                x1t, t_x1t, x1i = x1_r.next()
                dma("sp", x1t, x1s[tile * 128:(tile + 1) * 128, :], ("x1ld", x1i), [], [t_x1t])
                z, t_z, zi = z2_r.next()
                for n_, b in enumerate((b0, b1)):
                    stt(z[:, n_ * 512:(n_ + 1) * 512], x1t[:, n_ * 512:(n_ + 1) * 512], ALPHA, bank(b), ALU.mult, ALU.add,
                        [t_x1t, bank_t[b]], [t_z])
                layer_norm(z, t_z, ln2, t_ln2)
                t_y = Trk()
                dma("sp", y[tile * 128:(tile + 1) * 128, :], z, ("yst", zi), [t_z], [t_y])

        P.barrier()

        P.resolve()
        keys = [("eng", e) for e in ENGS] + [("dma", k) for k in P.dma_counts]
        sems = {}
        for i, k in enumerate(keys):
            sems[k] = es.enter_context(nc.semaphore("s%d" % i))
        P.emit(nc, sems)
    return nc


def _kmajor(w, kc):
    n = w.shape[1]
    return np.ascontiguousarray(w.reshape(kc, 128, n).transpose(1, 0, 2))


def _rope_tables(perm):
    t = np.arange(S)
    rows = (t // 64).astype(np.float32)
    cols = (t % 64).astype(np.float32)
    inv_freq = (np.float32(10000.0) ** (-np.arange(0, 16, 2, dtype=np.float32) / np.float32(16))).astype(np.float32)
    ang_r = rows[:, None] * inv_freq[None, :]
    ang_c = cols[:, None] * inv_freq[None, :]
    cr, sr, cc, sc = np.cos(ang_r), np.sin(ang_r), np.cos(ang_c), np.sin(ang_c)
    cosT = np.concatenate([cr, cr, cc, cc], axis=1)
    sinS = np.concatenate([-sr, sr, -sc, sc], axis=1)
    tab = np.stack([cosT[perm].T, sinS[perm].T], axis=0).astype(np.float32)
    return np.ascontiguousarray(tab)


_SWAP = np.concatenate([np.arange(8, 16), np.arange(0, 8), np.arange(24, 32), np.arange(16, 24)])


def _na_bias(rpb, half):
    out = np.empty((8, 128, 13, 128), np.float32)
    kk = np.arange(128)
    qq = np.arange(128)
    ti = 0
    for m in (0, 1, 2):
        chunks, pt0 = na_chunks(m)
        assert pt0 == ti
        for j in chunks:
            kr_l = 2 * j + kk // 64
            kc = kk % 64
            qr_l = 2 * m + qq // 64
            qc = qq % 64
            kr = kr_l if half == 0 else 63 - kr_l
            qr = qr_l if half == 0 else 63 - qr_l
            rs = np.clip(qr - 4, 0, 56)
            cs = np.clip(qc - 8, 0, 48)
            valid = ((kr[:, None] >= rs[None, :]) & (kr[:, None] < rs[None, :] + 8) &
                     (kc[:, None] >= cs[None, :]) & (kc[:, None] < cs[None, :] + 16))
            ri = np.clip(kr[:, None] - qr[None, :] + 7, 0, 14)
            ci = np.clip(kc[:, None] - qc[None, :] + 15, 0, 30)
            g = rpb[:, ri, ci]
            out[:, :, ti, :] = np.where(valid[None], g, np.float32(MASK_NEG))
            ti += 1
    return out


_NC_CACHE = {}


def _prep_inputs(x, w_in, b_gate, na_rpb, mla_q_norm, mla_w_uq, mla_kv_norm, mla_w_ukv,
                 w_branch_na, w_branch_mla, w_out, ln1_g, ln1_b, w_ffn_in, w_ffn_out, ln2_g, ln2_b):
    f32 = np.float32
    x = np.asarray(x, f32)
    w_in = np.asarray(w_in, f32)[0]
    wa = np.concatenate([w_in[:, 0:1952], w_in[:, 1920:1952][:, _SWAP]], axis=1)
    shared = {
        "w_in_a": _kmajor(wa, 8),
        "w_in_g": _kmajor(w_in[:, 1952:4000], 8),
        "wbna": _kmajor(np.asarray(w_branch_na, f32)[0], 4),
        "wbmla": _kmajor(np.asarray(w_branch_mla, f32)[0], 4),
        "wout": _kmajor(np.asarray(w_out, f32)[0], 8),
        "wf2": _kmajor(np.asarray(w_ffn_out, f32)[0], NJ),
        "ident": np.eye(128, dtype=f32),
    }
    wuq = np.asarray(mla_w_uq, f32)[0].reshape(256, 8, 96)
    nope, pe = wuq[:, :, 0:64], wuq[:, :, 64:96]
    wuq_r = np.concatenate([nope, pe, pe[:, :, _SWAP]], axis=2)
    shared["w_uq"] = np.ascontiguousarray(wuq_r.reshape(2, 128, 8, 128).transpose(1, 0, 2, 3))
    wukv = np.asarray(mla_w_ukv, f32)[0].reshape(128, 8, 128)
    shared["w_uk"] = np.ascontiguousarray(wukv[:, :, 0:64])
    shared["w_uv"] = np.ascontiguousarray(wukv[:, :, 64:128].reshape(128, 512))
    small = np.zeros((128, 32), f32)
    small[:, 0:2] = np.asarray(mla_q_norm, f32)[0].reshape(2, 128).T
    small[:, 2] = np.asarray(mla_kv_norm, f32)[0]
    small[:, 3:19] = np.asarray(b_gate, f32)[0].reshape(16, 128).T
    shared["small"] = small
    shared["ln"] = np.ascontiguousarray(np.stack([np.asarray(a, f32)[0] for a in (ln1_g, ln1_b, ln2_g, ln2_b)], 0))
    wfi = np.asarray(w_ffn_in, f32)[0]
    g = wfi[:, 0:FH].reshape(8, 128, NJ, 128)
    u = wfi[:, FH:2 * FH].reshape(8, 128, NJ, 128)
    wf1 = np.concatenate([g, u], axis=3)
    shared["wf1"] = np.ascontiguousarray(wf1.transpose(2, 1, 0, 3))
    rpb = np.asarray(na_rpb, f32)[0]
    perms = []
    t = np.arange(S)
    perms.append(t)
    perms.append((63 - t // 64) * 64 + t % 64)
    half_in = []
    for half in range(2):
        half_in.append({"rope": _rope_tables(perms[half]), "nab": _na_bias(rpb, half)})
    in_maps = []
    for core in range(8):
        b, half = core // 2, core % 2
        d = dict(shared)
        d.update(half_in[half])
        d["x"] = np.ascontiguousarray(x[b][perms[half]])
        in_maps.append(d)
    return in_maps, perms


def kernel(**inputs):
    in_maps, perms = _prep_inputs(**inputs)
    if "nc" not in _NC_CACHE:
        _NC_CACHE["nc"] = build_nc(False)
    nc = _NC_CACHE["nc"]
    res = run_bass_kernel_spmd(nc, in_maps, core_ids=list(range(8)))
    out = np.empty((4, S, D), np.float32)
    for core in range(8):
        b, half = core // 2, core % 2
        out[b, perms[half][:NT]] = res.results[core]["y"]
    return out
```
